# Optimizing a Trainium2 kernel written in Bass

```python
import jax
import jax.numpy as jnp
from jax import lax
import numpy as np

D_MODEL = 2048
BATCH = 16
SEQ = 2048
DEPTH = 4

GRID_W = 64
CTX_LEN = 256
N_MIXERS = 3
N_MOD = 6
EPS = 1e-6

HEAD_DIM = 128
N_Q_HEADS = D_MODEL // HEAD_DIM
N_KV_HEADS = max(N_Q_HEADS // 4, 1)
Q_PER_KV = N_Q_HEADS // N_KV_HEADS
WINDOW = 128
BLOCK = 128
ROPE_BASE = 10000.0

GMLP_CHUNK = 128
GMLP_WIDTH = 2 * D_MODEL
GMLP_GROUPS = 16

POOL_SIZES = (2, 4, 8, 16)
POOL_GROUPS = len(POOL_SIZES)

D_FF = 4 * D_MODEL

N_ATTN = (DEPTH + 2) // 3
N_GMLP = (DEPTH + 1) // 3
N_POOL = DEPTH // 3

kernel_name = 'hybrid_interleaved_diffusion_backbone'


def rmsnorm(x, g):
    xf = x.astype(jnp.float32)
    y = xf * lax.rsqrt(jnp.mean(xf * xf, axis=-1, keepdims=True) + EPS)
    return (y * g.astype(jnp.float32)).astype(x.dtype)


def layernorm(x, g):
    xf = x.astype(jnp.float32)
    xc = xf - jnp.mean(xf, axis=-1, keepdims=True)
    y = xc * lax.rsqrt(jnp.mean(xc * xc, axis=-1, keepdims=True) + EPS)
    return (y * g.astype(jnp.float32)).astype(x.dtype)


def axial_rope_tables(n_tokens):
    n_rows = n_tokens // GRID_W
    row = jnp.broadcast_to(jnp.arange(n_rows)[:, None], (n_rows, GRID_W)).reshape(-1)
    col = jnp.broadcast_to(jnp.arange(GRID_W)[None, :], (n_rows, GRID_W)).reshape(-1)
    axis_dim = HEAD_DIM // 2
    inv_freq = ROPE_BASE ** (-jnp.arange(0, axis_dim, 2, dtype=jnp.float32) / axis_dim)
    ang = jnp.stack([row.astype(jnp.float32)[:, None] * inv_freq,
                     col.astype(jnp.float32)[:, None] * inv_freq], axis=1)
    return jnp.cos(ang), jnp.sin(ang)


def apply_axial_rope(x, cos, sin):
    q = HEAD_DIM // 4
    xr = x.reshape(x.shape[:-1] + (2, 2, q))
    x1, x2 = xr[..., 0, :], xr[..., 1, :]
    bshape = (cos.shape[0],) + (1,) * (x.ndim - 3) + (2, q)
    cs = cos.reshape(bshape).astype(x.dtype)
    sn = sin.reshape(bshape).astype(x.dtype)
    out = jnp.stack([x1 * cs - x2 * sn, x2 * cs + x1 * sn], axis=-2)
    return out.reshape(x.shape)


def windowed_gqa_sink(h, hc, wq, wk, wv, wo, sink, with_ctx_out):
    B, S, _ = h.shape
    L = hc.shape[1]
    nb = S // BLOCK
    scale = HEAD_DIM ** -0.5
    cos, sin = axial_rope_tables(S)
    q = apply_axial_rope((h @ wq).reshape(B, S, N_KV_HEADS, Q_PER_KV, HEAD_DIM), cos, sin)
    k = apply_axial_rope((h @ wk).reshape(B, S, N_KV_HEADS, HEAD_DIM), cos, sin)
    v = (h @ wv).reshape(B, S, N_KV_HEADS, HEAD_DIM)
    kc = (hc @ wk).reshape(B, L, N_KV_HEADS, HEAD_DIM)
    vc = (hc @ wv).reshape(B, L, N_KV_HEADS, HEAD_DIM)
    sink_f = sink.astype(jnp.float32).reshape(N_KV_HEADS, Q_PER_KV, 1, 1)

    def band(t):
        tp = jnp.pad(t, ((0, 0), (BLOCK, BLOCK), (0, 0), (0, 0)))
        tp = tp.reshape(B, nb + 2, BLOCK, N_KV_HEADS, HEAD_DIM)
        tb = jnp.concatenate([tp[:, :-2], tp[:, 1:-1], tp[:, 2:]], axis=2)
        return jnp.moveaxis(tb, 1, 0)

    qb = jnp.moveaxis(q.reshape(B, nb, BLOCK, N_KV_HEADS, Q_PER_KV, HEAD_DIM), 1, 0)
    kb, vb = band(k), band(v)
    qi = jnp.arange(BLOCK)[:, None]
    kj = jnp.arange(3 * BLOCK)[None, :]
    key_pos = (jnp.arange(nb)[:, None, None] - 1) * BLOCK + kj[None]
    valid = (jnp.abs(kj - BLOCK - qi)[None] <= WINDOW) & (key_pos >= 0) & (key_pos < S)

    def attend_block(args):
        q_blk, k_blk, v_blk, mask = args
        s_win = jnp.einsum('bqhgd,bkhd->bhgqk', q_blk, k_blk).astype(jnp.float32) * scale
        s_win = jnp.where(mask, s_win, -jnp.inf)
        s_ctx = jnp.einsum('bqhgd,bkhd->bhgqk', q_blk, kc).astype(jnp.float32) * scale
        s_sink = jnp.broadcast_to(sink_f, s_win.shape[:-1] + (1,))
        p = jax.nn.softmax(jnp.concatenate([s_win, s_ctx, s_sink], axis=-1), axis=-1).astype(v_blk.dtype)
        o = jnp.einsum('bhgqk,bkhd->bqhgd', p[..., :3 * BLOCK], v_blk)
        return o + jnp.einsum('bhgqk,bkhd->bqhgd', p[..., 3 * BLOCK:3 * BLOCK + L], vc)

    o = lax.map(attend_block, (qb, kb, vb, valid))
    y = jnp.moveaxis(o, 0, 1).reshape(B, S, N_Q_HEADS * HEAD_DIM) @ wo
    if not with_ctx_out:
        return y, None
    qc = (hc @ wq).reshape(B, L, N_KV_HEADS, Q_PER_KV, HEAD_DIM)
    sc = jnp.einsum('bqhgd,bkhd->bhgqk', qc, kc).astype(jnp.float32) * scale
    sc = jnp.concatenate([sc, jnp.broadcast_to(sink_f, sc.shape[:-1] + (1,))], axis=-1)
    pc = jax.nn.softmax(sc, axis=-1).astype(vc.dtype)
    oc = jnp.einsum('bhgqk,bkhd->bqhgd', pc[..., :L], vc)
    return y, oc.reshape(B, L, N_Q_HEADS * HEAD_DIM) @ wo


def chunk_gmlp(h, w_in, g_v, w_s, b_s, w_out):
    B, S, _ = h.shape
    nc = S // GMLP_CHUNK
    cg = GMLP_WIDTH // GMLP_GROUPS
    uv = jax.nn.gelu(h @ w_in)
    u, v = uv[..., :GMLP_WIDTH], uv[..., GMLP_WIDTH:]
    v = layernorm(v, g_v).reshape(B, nc, GMLP_CHUNK, GMLP_GROUPS, cg)
    sv = jnp.einsum('gpq,bnqgc->bnpgc', w_s, v) + jnp.transpose(b_s)[None, None, :, :, None]
    return (u * sv.reshape(B, S, GMLP_WIDTH)) @ w_out


def multiscale_pool(h, w_pool, scale):
    B, S, D = h.shape
    cg = D // POOL_GROUPS
    csum = jnp.pad(jnp.cumsum(h.astype(jnp.float32), axis=1), ((0, 0), (1, 0), (0, 0)))
    t = jnp.arange(S)
    pooled = []
    for gi, w in enumerate(POOL_SIZES):
        lo = jnp.clip(t - w // 2, 0, S)
        hi = jnp.clip(t + w // 2, 0, S)
        cs_g = csum[:, :, gi * cg:(gi + 1) * cg]
        cnt = (hi - lo).astype(jnp.float32)[None, :, None]
        pooled.append((cs_g[:, hi] - cs_g[:, lo]) / cnt)
    d = jnp.concatenate(pooled, axis=-1).astype(h.dtype) - h
    y = jnp.einsum('bsgc,gcd->bsgd', d.reshape(B, S, POOL_GROUPS, cg), w_pool)
    return y.reshape(B, S, D) * scale


def sqrelu_mlp(h, w_up, w_down):
    return jnp.square(jax.nn.relu(h @ w_up)) @ w_down


def setup_inputs(seed: int = 0) -> dict:
    key = jax.random.key(seed)
    ks = jax.random.split(key, 22)
    D = D_MODEL
    HQD = N_Q_HEADS * HEAD_DIM
    HKD = N_KV_HEADS * HEAD_DIM
    cg = D // POOL_GROUPS

    def normal(k, shape, std=1.0):
        return jax.random.normal(k, shape, jnp.float32) * std

    return {
        'x': normal(ks[0], (BATCH, SEQ, D)),
        'c': normal(ks[1], (BATCH, D)),
        'ctx': normal(ks[2], (BATCH, CTX_LEN, D)),
        'c_ctx': normal(ks[3], (D,)),
        'w_mod': normal(ks[4], (DEPTH, D, N_MOD * D), 0.5 * D ** -0.5),
        'b_mod': normal(ks[5], (DEPTH, N_MOD * D), 0.02),
        'g_norm': 1.0 + normal(ks[6], (DEPTH, 2, D), 0.05),
        'w_up': normal(ks[7], (DEPTH, D, D_FF), D ** -0.5),
        'w_down': normal(ks[8], (DEPTH, D_FF, D), D_FF ** -0.5),
        'attn_wq': normal(ks[9], (N_ATTN, D, HQD), D ** -0.5),
        'attn_wk': normal(ks[10], (N_ATTN, D, HKD), D ** -0.5),
        'attn_wv': normal(ks[11], (N_ATTN, D, HKD), D ** -0.5),
        'attn_wo': normal(ks[12], (N_ATTN, HQD, D), HQD ** -0.5),
        'attn_sink': normal(ks[13], (N_ATTN, N_Q_HEADS), 0.5),
        'gmlp_w_in': normal(ks[14], (N_GMLP, D, 2 * GMLP_WIDTH), D ** -0.5),
        'gmlp_g_v': 1.0 + normal(ks[15], (N_GMLP, GMLP_WIDTH), 0.05),
        'gmlp_w_s': normal(ks[16], (N_GMLP, GMLP_GROUPS, GMLP_CHUNK, GMLP_CHUNK), GMLP_CHUNK ** -0.5),
        'gmlp_b_s': 1.0 + normal(ks[17], (N_GMLP, GMLP_GROUPS, GMLP_CHUNK), 0.1),
        'gmlp_w_out': normal(ks[18], (N_GMLP, GMLP_WIDTH, D), GMLP_WIDTH ** -0.5),
        'pool_w': normal(ks[19], (N_POOL, POOL_GROUPS, cg, cg), cg ** -0.5),
        'pool_scale': 1.0 + normal(ks[20], (N_POOL, D), 0.1),
        'g_final': 1.0 + normal(ks[21], (D,), 0.05),
    }


def reference(x, c, ctx, c_ctx, w_mod, b_mod, g_norm, w_up, w_down,
              attn_wq, attn_wk, attn_wv, attn_wo, attn_sink,
              gmlp_w_in, gmlp_g_v, gmlp_w_s, gmlp_b_s, gmlp_w_out,
              pool_w, pool_scale, g_final):
    B = x.shape[0]
    D = x.shape[-1]
    s_lat = jax.nn.silu(c)
    s_ctx = jax.nn.silu(c_ctx)
    for i in range(DEPTH):
        kind, j = i % N_MIXERS, i // N_MIXERS
        last = i == DEPTH - 1
        need_ctx = (not last) or kind == 0
        m = (s_lat @ w_mod[i] + b_mod[i]).reshape(B, N_MOD, 1, D)
        sh_a, sc_a, gt_a, sh_m, sc_m, gt_m = (m[:, k] for k in range(N_MOD))
        h = rmsnorm(x, g_norm[i, 0]) * (1 + sc_a) + sh_a
        hc = None
        if need_ctx:
            mc = (s_ctx @ w_mod[i] + b_mod[i]).reshape(N_MOD, 1, 1, D)
            csh_a, csc_a, cgt_a, csh_m, csc_m, cgt_m = (mc[k] for k in range(N_MOD))
            hc = rmsnorm(ctx, g_norm[i, 0]) * (1 + csc_a) + csh_a
        if kind == 0:
            y, yc = windowed_gqa_sink(h, hc, attn_wq[j], attn_wk[j], attn_wv[j], attn_wo[j],
                                      attn_sink[j], not last)
        elif kind == 1:
            y = chunk_gmlp(h, gmlp_w_in[j], gmlp_g_v[j], gmlp_w_s[j], gmlp_b_s[j], gmlp_w_out[j])
            yc = None if last else chunk_gmlp(hc, gmlp_w_in[j], gmlp_g_v[j], gmlp_w_s[j],
                                              gmlp_b_s[j], gmlp_w_out[j])
        else:
            y = multiscale_pool(h, pool_w[j], pool_scale[j])
            yc = None if last else multiscale_pool(hc, pool_w[j], pool_scale[j])
        x = x + gt_a * y
        x = x + gt_m * sqrelu_mlp(rmsnorm(x, g_norm[i, 1]) * (1 + sc_m) + sh_m, w_up[i], w_down[i])
        if not last:
            ctx = ctx + cgt_a * yc
            ctx = ctx + cgt_m * sqrelu_mlp(rmsnorm(ctx, g_norm[i, 1]) * (1 + csc_m) + csh_m,
                                           w_up[i], w_down[i])
    return rmsnorm(x, g_final)
```

```python
from contextlib import ExitStack
import numpy as np
import concourse.bass as bass
import concourse.mybir as mybir
from concourse.bass_utils import run_bass_kernel_spmd

F32 = mybir.dt.float32
BF16 = mybir.dt.bfloat16
AF = mybir.ActivationFunctionType
ALU = mybir.AluOpType
AX = mybir.AxisListType

D = 2048
KC = 16
S = 2048
L = 256
SL = S + L
NLAT = 2 * S
NTOK = 2 * SL
DFF = 8192
GW = 4096
EPS = 1e-6
NEG = -30000.0
PAGE = 512
ARENA_WORDS = 53200


class _Op:
    __slots__ = ("stream", "fn", "deps", "marked", "tick", "epoch", "dma", "slot", "sval")


class Sched:
    EPOCH = 30000
    NSLOT = 8

    def __init__(self):
        self.ops = []
        self.lw = {}
        self.rd = {}
        self.dma_count = {}

    @staticmethod
    def _keys(items):
        out = []
        for it in items:
            if isinstance(it, tuple) and len(it) == 3 and it[0] == "sb":
                for pg in range(it[1] // PAGE, (it[2] - 1) // PAGE + 1):
                    out.append(("sb", pg))
            else:
                out.append(it)
        return out

    def add(self, stream, fn, reads=(), writes=(), dma=False):
        i = len(self.ops)
        op = _Op()
        op.stream, op.fn, op.dma, op.marked = stream, fn, dma, False
        op.tick = op.epoch = op.slot = op.sval = 0
        rk = self._keys(reads)
        wk = self._keys(writes)
        deps = set()
        for k in rk:
            w = self.lw.get(k)
            if w is not None:
                deps.add(w)
            if isinstance(k, tuple) and k[0] == "ps":
                r = self.rd.get(k)
                if r:
                    deps.update(v for s_, v in r.items() if s_ != stream)
        for k in wk:
            w = self.lw.get(k)
            if w is not None:
                deps.add(w)
            r = self.rd.get(k)
            if r:
                deps.update(r.values())
        if stream == "pe" and not dma:
            deps = {d for d in deps if self.ops[d].dma or self.ops[d].stream != "pe"}
        for d in deps:
            self.ops[d].marked = True
        op.deps = deps
        rkey = ("d", i) if dma else stream
        for k in rk:
            self.rd.setdefault(k, {})[rkey] = i
        for k in wk:
            self.lw[k] = i
            self.rd[k] = {}
        if dma:
            n = self.dma_count.get(stream, 0)
            self.dma_count[stream] = n + 1
            op.slot = n % self.NSLOT
            op.sval = 16 * (n // self.NSLOT + 1)
        self.ops.append(op)
        return i

    def emit(self, nc, es):
        streams = ("pe", "act", "dve", "pool", "sp")
        cnt = {s: 0 for s in streams}
        for op in self.ops:
            if op.marked and not op.dma:
                n = cnt[op.stream]
                op.epoch, op.tick = n // self.EPOCH, n % self.EPOCH + 1
                cnt[op.stream] = n + 1
        csem = {s: [es.enter_context(nc.semaphore(f"c_{s}_{e}")) for e in range(cnt[s] // self.EPOCH + 1)]
                for s in streams}
        dsem = {s: [es.enter_context(nc.semaphore(f"d_{s}_{j}")) for j in range(self.NSLOT)]
                for s in self.dma_count}
        block = es.enter_context(nc.Block())
        ops = self.ops

        def run(stream, eng):
            waited = {}
            for op in ops:
                if op.stream != stream:
                    continue
                need = {}
                for d in op.deps:
                    X = ops[d]
                    if X.dma:
                        key, val = ("d", X.stream, X.slot), X.sval
                    else:
                        key, val = ("c", X.stream, X.epoch), X.tick
                    if need.get(key, 0) < val:
                        need[key] = val
                if op.dma and op.sval > 16:
                    key = ("d", stream, op.slot)
                    if need.get(key, 0) < op.sval - 16:
                        need[key] = op.sval - 16
                for key, val in need.items():
                    if waited.get(key, 0) < val:
                        sem = dsem[key[1]][key[2]] if key[0] == "d" else csem[key[1]][key[2]]
                        eng.wait_ge(sem, val)
                        waited[key] = val
                ins = op.fn(eng)
                if ins is None:
                    continue
                if op.dma:
                    ins.then_inc(dsem[stream][op.slot], 16)
                elif op.marked:
                    ins.then_inc(csem[stream][op.epoch], 1)

        @block.tensor
        def _(e):
            run("pe", e)

        @block.scalar
        def _(e):
            run("act", e)

        @block.vector
        def _(e):
            run("dve", e)

        @block.gpsimd
        def _(e):
            run("pool", e)

        @block.sync
        def _(e):
            run("sp", e)


class V:
    def __init__(self, ap, off, nb):
        self.ap, self.off, self.nb = ap, off, nb

    def all(self):
        return ("sb", self.off, self.off + self.nb)

    def sub(self, k, n=1):
        s = self.nb // self.ap.shape[1]
        return ("sb", self.off + k * s, self.off + (k + n) * s)


class Arena:
    def __init__(self, nc, es):
        self.t = es.enter_context(nc.sbuf_tensor("arena", [128, ARENA_WORDS], F32))
        self.top = 0

    def alloc(self, shape, dt):
        esz = 4 if dt == F32 else 2
        n = int(np.prod(shape))
        nb = (n * esz + 63) // 64 * 64
        if nb >= 2048:
            self.top = (self.top + PAGE - 1) // PAGE * PAGE
            nb = (nb + PAGE - 1) // PAGE * PAGE
        off = self.top
        self.top += nb
        assert self.top <= ARENA_WORDS * 4, f"SBUF overflow {self.top}"
        return self.view(off, shape, dt)

    def view(self, off, shape, dt):
        esz = 4 if dt == F32 else 2
        n = int(np.prod(shape))
        nb = (n * esz + 63) // 64 * 64
        assert off % 4 == 0 and off + nb <= ARENA_WORDS * 4
        self.last_nb = nb
        ap = self.t[:, off // 4:(off + nb) // 4]
        if dt != F32:
            ap = ap.bitcast(dt)
        ap = ap[:, 0:n]
        if len(shape) == 2:
            ap = ap.rearrange("p (a b) -> p a b", a=shape[0])
        elif len(shape) == 3:
            ap = ap.rearrange("p (a b c) -> p a b c", a=shape[0], b=shape[1])
        return V(ap, off, nb)


def tgs_of(T):
    out, o = [], 0
    while o < T:
        sz = min(512, T - o)
        out.append((o, sz))
        o += sz
    return out


class Builder:
    def __init__(self, nlayers=4, dbg=False, do_final=True):
        self.nlayers, self.dbg, self.do_final = nlayers, dbg, do_final
        self.es = ExitStack()
        nc = self.nc = bass.Bass("TRN2", target_bir_lowering=False)
        self.sc = Sched()
        dt = nc.dram_tensor

        def ein(name, shape):
            return dt(name, shape, F32, kind="ExternalInput").ap()

        self.x_in = ein("x", [NLAT, D])
        self.ctx_in = ein("ctx", [2 * L, D])
        self.cc_in = ein("cc", [128, KC, 3])
        self.w_mod = ein("w_mod", [4, D, 6 * D])
        self.b_mod = ein("b_mod", [4, 6 * D])
        self.g_norm = ein("g_norm", [4, 2, D])
        self.w_up = ein("w_up", [4, D, DFF])
        self.w_down = ein("w_down", [4, DFF, D])
        self.attn_wq = ein("attn_wq", [2, D, D])
        self.attn_wk = ein("attn_wk", [2, D, 512])
        self.attn_wv = ein("attn_wv", [2, D, 512])
        self.attn_wo = ein("attn_wo", [2, D, D])
        self.attn_sink = ein("attn_sink", [2, 16])
        self.gmlp_w_in = ein("gmlp_w_in", [1, D, 2 * GW])
        self.gmlp_g_v = ein("gmlp_g_v", [1, GW])
        self.gmlp_w_s = ein("gmlp_w_s", [1, 16, 128, 128])
        self.gmlp_b_s = ein("gmlp_b_s", [1, 16, 128])
        self.gmlp_w_out = ein("gmlp_w_out", [1, GW, D])
        self.pool_w = ein("pool_w", [1, 4, 512, 512])
        self.pool_scale = ein("pool_scale", [1, D])
        self.g_final = ein("g_final", [D])
        self.k_ident = ein("k_ident", [128, 128])
        self.k_rot = ein("k_rot", [128, 128])
        self.k_mask = ein("k_mask", [128, 384])
        self.k_rope = ein("k_rope", [4, 128, SL])
        self.k_rc = ein("k_rc", [4, S + L])
        self.out = dt("out", [NLAT, D], F32, kind="ExternalOutput").ap()
        xk = "ExternalOutput" if dbg else "Internal"
        self.xT_d = dt("xT_d", [KC, 128, NTOK], F32, kind=xk).ap()
        self.QT_d = dt("QT_d", [2, 16, 128, SL], BF16).ap()
        self.KT_d = dt("KT_d", [2, 4, 128, SL], BF16).ap()
        self.V_d = dt("V_d", [2, SL, 512], BF16).ap()
        self.OT_d = dt("OT_d", [2, 16, 128, SL], BF16).ap()

        self.A = Arena(nc, self.es)
        self.ps = [self.es.enter_context(nc.psum_tensor(f"ps{i}", [128, 512], F32)) for i in range(8)]
        self.ps_next = 0
        self.nslot_w = 3
        self.add_eng = "dve"
        self.bg = None
        self.bg_n = 0
        self.bg_evac_due = False
        self.wslot_i = 0

    def op(self, stream, fn, reads=(), writes=(), dma=False):
        return self.sc.add(stream, fn, reads, writes, dma)

    def bank(self):
        i = self.ps_next
        self.ps_next = (i + 1) % 8
        return i

    def dma(self, stream, out, in_, reads, writes, slow=False):
        if slow:
            self.op(stream, lambda e: e.dma_start(out=out, in_=in_, allow_slow_non_contiguous=True), reads, writes, dma=True)
        else:
            self.op(stream, lambda e: e.dma_start(out=out, in_=in_), reads, writes, dma=True)

    def setup_consts(self):
        A, nc = self.A, self.nc
        self.ident_f = A.alloc([128], F32)
        self.ident_b = A.alloc([128], BF16)
        self.rot_b = A.alloc([128], BF16)
        self.ones_b = A.alloc([128], BF16)
        self.mask_b = A.alloc([384], BF16)
        self.sT = A.alloc([KC, 3], F32)
        self.modv = [A.alloc([96, 3], F32) for _ in range(4)]
        self.g1 = [[A.alloc([KC, 3], F32) for _ in range(2)] for _ in range(4)]
        self.gn = A.alloc([8, KC], F32)
        self.gfin = A.alloc([KC], F32)
        self.pscale = A.alloc([KC], F32)
        self.gts = A.alloc([KC, 3], F32)
        self.gv = A.alloc([32], F32)
        self.esink = A.alloc([2, 16], F32)
        self.bmod = A.alloc([4, 96], F32)
        self.wslots = [A.alloc([KC, 256], BF16) for _ in range(self.nslot_w)]

        i_f, i_b, r_b, m_b, o_b = self.ident_f, self.ident_b, self.rot_b, self.mask_b, self.ones_b
        self.dma("sp", i_f.ap, self.k_ident, [], [i_f.all()])
        self.dma("pool", i_b.ap, self.k_ident, [], [i_b.all()])
        self.dma("pool", r_b.ap, self.k_rot, [], [r_b.all()])
        self.dma("pool", m_b.ap, self.k_mask, [], [m_b.all()])
        self.op("dve", lambda e: e.memset(o_b.ap, 1.0), [], [o_b.all()])
        self.epsD = A.alloc([16], F32)
        epsD = self.epsD
        self.op("dve", lambda e: e.memset(epsD.ap, float(EPS * D)), [], [epsD.all()])
        self.epsv = A.alloc([16], F32)
        epsv = self.epsv
        self.op("dve", lambda e: e.memset(epsv.ap, float(EPS)), [], [epsv.all()])
        gn = self.gn
        for j in range(8):
            src = self.g_norm[j // 2, j % 2].rearrange("(k p) -> p k", p=128)
            self.dma("sp", gn.ap[:, j, :], src, [], [gn.all()], slow=True)
        self.dma("sp", self.gfin.ap, self.g_final.rearrange("(k p) -> p k", p=128), [], [self.gfin.all()], slow=True)
        gf_ = self.gfin
        self.op("act", lambda e: e.mul(gf_.ap, gf_.ap, float(np.sqrt(D))), [gf_.all()], [gf_.all()])
        self.dma("sp", self.pscale.ap, self.pool_scale[0].rearrange("(k p) -> p k", p=128), [], [self.pscale.all()], slow=True)
        self.dma("sp", self.gv.ap, self.gmlp_g_v[0].rearrange("(k p) -> p k", p=128), [], [self.gv.all()], slow=True)
        for i in range(4):
            self.dma("sp", self.bmod.ap[:, i, :], self.b_mod[i].rearrange("(k p) -> p k", p=128), [], [self.bmod.all()], slow=True)
        es_ = self.esink
        for j in range(2):
            src = self.attn_sink[j:j + 1, :].broadcast_to([128, 16])
            self.dma("sp", es_.ap[:, j, :], src, [], [es_.all()], slow=True)
        self.op("act", lambda e: e.activation(out=es_.ap, in_=es_.ap, func=AF.Exp), [es_.all()], [es_.all()])
        sT = self.sT
        self.dma("sp", sT.ap, self.cc_in, [], [sT.all()])
        self.op("act", lambda e: e.activation(out=sT.ap, in_=sT.ap, func=AF.Silu), [sT.all()], [sT.all()])
        self.s_hl = A.alloc([KC, 6], BF16)
        shl = self.s_hl
        self.op("dve", lambda e: e.tensor_copy(out=shl.ap[:, :, 0:3], in_=sT.ap), [sT.all()], [shl.all()])
        self.op("dve", lambda e: e.tensor_tensor(out=shl.ap[:, :, 3:6], in0=sT.ap, in1=shl.ap[:, :, 0:3], op=ALU.subtract),
                [sT.all(), shl.all()], [shl.all()])

    def mod_steps(self, i):
        modv, shl, bm = self.modv[i], self.s_hl, self.bmod
        pend = None
        for c in range(48):
            sl = self.wslots[self.wslot_i % self.nslot_w]
            self.wslot_i += 1
            src = self.w_mod[i][:, c * 256:(c + 1) * 256].rearrange("(k p) m -> p k m", p=128)
            self.dma("pool", sl.ap, src, [], [sl.all()])
            b = self.bank()
            ps = self.ps[b]

            def mm(e, sl=sl, ps=ps):
                for m in range(2):
                    for h in range(2):
                        for k in range(KC):
                            ins = e.matmul(ps[:, m * 3:(m + 1) * 3], sl.ap[:, k, m * 128:(m + 1) * 128], shl.ap[:, k, h * 3:(h + 1) * 3],
                                           start=(h == 0 and k == 0), stop=(h == 1 and k == KC - 1))
                return ins
            self.op("pe", mm, [sl.all(), shl.all()], [("ps", b)])
            yield
            self.op("dve", lambda e, ps=ps, c=c: e.tensor_tensor(out=modv.ap[:, c * 2:(c + 1) * 2, :],
                                                                 in0=ps[:, 0:6].rearrange("p (m r) -> p m r", r=3),
                                                                 in1=bm.ap[:, i, c * 2:(c + 1) * 2].unsqueeze(2).broadcast_to([128, 2, 3]),
                                                                 op=ALU.add), [("ps", b), bm.all()], [modv.all()])
            yield
        for which in range(2):
            g1 = self.g1[i][which]
            j = 1 + 3 * which
            gn = self.gn
            self.op("dve", lambda e, g1=g1, j=j: e.tensor_scalar(
                out=g1.ap, in0=modv.ap[:, j * 16:(j + 1) * 16, :], scalar1=1.0, scalar2=float(np.sqrt(D)),
                op0=ALU.add, op1=ALU.mult), [modv.all()], [g1.all()])
            self.op("dve", lambda e, g1=g1, gn=gn, which=which: e.tensor_tensor(
                out=g1.ap, in0=g1.ap, in1=gn.ap[:, i * 2 + which, :].unsqueeze(2).broadcast_to([128, KC, 3]), op=ALU.mult),
                [gn.all(), g1.all()], [g1.all()])
        if i == 2:
            gts, psc = self.gts, self.pscale
            self.op("dve", lambda e: e.tensor_tensor(
                out=gts.ap, in0=modv.ap[:, 32:48, :], in1=psc.ap.unsqueeze(2).broadcast_to([128, KC, 3]), op=ALU.mult),
                [modv.all(), psc.all()], [gts.all()])

    def modulation(self, layers):
        for i in layers:
            for _ in self.mod_steps(i):
                pass

    def bg_tick(self):
        if self.bg is None:
            return
        self.bg_n += 1
        if not self.bg_evac_due and self.bg_n % 3:
            return
        try:
            next(self.bg)
            self.bg_evac_due = not self.bg_evac_due
        except StopIteration:
            self.bg = None
            self.bg_evac_due = False

    def bg_flush(self):
        if self.bg is not None:
            for _ in self.bg:
                pass
            self.bg = None
        self.bg_evac_due = False

    def load_inputs(self):
        A = self.A
        mark = A.top
        xin = [A.alloc([D], F32) for _ in range(2)]
        stg = [A.alloc([KC, 512], F32) for _ in range(2)]
        idf = self.ident_f
        nt = 0
        for g in range(NTOK // 512):
            st = stg[g % 2]
            for q in range(4):
                t0 = g * 512 + q * 128
                xi = xin[nt % 2]
                nt += 1
                src = self.x_in[t0:t0 + 128, :] if t0 < NLAT else self.ctx_in[t0 - NLAT:t0 - NLAT + 128, :]
                self.dma("sp", xi.ap, src, [], [xi.all()])
                for kg in range(4):
                    b = self.bank()
                    ps = self.ps[b]

                    def tp(e, ps=ps, xi=xi, kg=kg):
                        for j in range(4):
                            k = kg * 4 + j
                            ins = e.transpose(ps[:, j * 128:(j + 1) * 128], xi.ap[:, k * 128:(k + 1) * 128], idf.ap)
                        return ins
                    self.op("pe", tp, [xi.all(), idf.all()], [("ps", b)])
                    eng = "act" if kg % 2 == 0 else "dve"
                    o = st.ap[:, kg * 4:(kg + 1) * 4, q * 128:(q + 1) * 128]
                    i_ = ps[:, :].rearrange("p (j t) -> p j t", j=4)
                    if eng == "act":
                        self.op("act", lambda e, o=o, i_=i_: e.activation(out=o, in_=i_, func=AF.Copy), [("ps", b)], [st.all()])
                    else:
                        self.op("dve", lambda e, o=o, i_=i_: e.tensor_copy(out=o, in_=i_), [("ps", b)], [st.all()])
            dst = self.xT_d[:, :, g * 512:(g + 1) * 512].rearrange("k p t -> p k t")
            self.dma("sp", dst, st.ap, [st.all()], [("xT", g * 4 + q) for q in range(4)])
        A.top = mark

    def xkeys(self, t0, T):
        return [("xT", j) for j in range(t0 // 128, (t0 + T - 1) // 128 + 1)]

    def norm_tile(self, xin, hT, t0, T, g1, shcol, r, rstd, sqb, h_f32=False, ncols=None, col0=0):
        ncols = T if ncols is None else ncols
        ones = self.ones_b
        groups = [(col0 + o, sz) for (o, sz) in tgs_of(ncols)]
        banks = [self.bank() for _ in groups]
        for k in range(KC):
            sq = sqb[k % 2]
            self.op("act", lambda e, sq=sq, k=k: e.activation(out=sq.ap[:, col0:col0 + ncols], in_=xin.ap[:, k, col0:col0 + ncols], func=AF.Square),
                    [xin.sub(k)], [sq.all()])
            for (o, sz), b in zip(groups, banks):
                ps = self.ps[b]
                self.op("pe", lambda e, ps=ps, sq=sq, o=o, sz=sz, k=k: e.matmul(ps[:, 0:sz], ones.ap, sq.ap[:, o:o + sz],
                                                                              start=(k == 0), stop=(k == KC - 1)),
                        [sq.all(), ones.all()], [("ps", b)])
        for (o, sz), b in zip(groups, banks):
            ps = self.ps[b]
            self.op("act", lambda e, ps=ps, o=o, sz=sz: e.activation(out=rstd.ap[:, o:o + sz], in_=ps[:, 0:sz], func=AF.Sqrt,
                                                                     bias=self.epsD.ap[:, 0:1]),
                    [("ps", b), self.epsD.all()], [rstd.all()])
            self.op("dve", lambda e, o=o, sz=sz: e.reciprocal(out=rstd.ap[:, o:o + sz], in_=rstd.ap[:, o:o + sz]),
                    [rstd.all()], [rstd.all()])
        for k in range(KC):
            xs = xin.ap[:, k, col0:col0 + ncols]
            self.op("dve", lambda e, xs=xs, k=k: e.scalar_tensor_tensor(out=xs, in0=xs, scalar=g1.ap[:, k, r:r + 1],
                                                                         in1=rstd.ap[:, col0:col0 + ncols], op0=ALU.mult, op1=ALU.mult),
                    [xin.sub(k), rstd.all(), g1.all()], [xin.sub(k)])
            self.op("act", lambda e, xs=xs, k=k: e.activation(out=hT.ap[:, k, col0:col0 + ncols], in_=xs, func=AF.Identity,
                                                               bias=shcol(k)),
                    [xin.sub(k)], [hT.sub(k)])

    def gemm_fm(self, w_ap, K, m0, M, rhs, rhs_reads, tgs, evac, bg=False):
        KQ = K // 2048
        for mp in range(M // 256):
            banks = [[self.bank() for _ in tgs] for _ in range(2)]
            for kq in range(KQ):
                sl = self.wslots[self.wslot_i % self.nslot_w]
                self.wslot_i += 1
                src = w_ap[kq * 2048:(kq + 1) * 2048, m0 + mp * 256:m0 + (mp + 1) * 256].rearrange("(k p) m -> p k m", p=128)
                self.dma("pool", sl.ap, src, [], [sl.all()])

                def mm(e, sl=sl, kq=kq, banks=banks):
                    for m in range(2):
                        for k in range(KC):
                            for (o, sz), b in zip(tgs, banks[m]):
                                ins = e.matmul(self.ps[b][:, 0:sz], sl.ap[:, k, m * 128:(m + 1) * 128], rhs(kq * KC + k, o, sz),
                                               start=(kq == 0 and k == 0), stop=(kq == KQ - 1 and k == KC - 1))
                    return ins
                self.op("pe", mm, [sl.all()] + rhs_reads(kq), [("ps", b) for bb in banks for b in bb])
            for m in range(2):
                evac(mp * 2 + m, [(b, o, sz) for (o, sz), b in zip(tgs, banks[m])])
            if bg:
                self.bg_tick()

    def residual_evac(self, t0, T, gate, xcb, cnt):
        def evac(n, parts):
            xc = xcb[cnt[0] % len(xcb)]
            cnt[0] += 1
            keys = self.xkeys(t0, T)
            self.dma("sp", xc.ap[:, 0:T], self.xT_d[n, :, t0:t0 + T], keys, [xc.all()])
            for (b, o, sz) in parts:
                ps = self.ps[b]
                self.op("dve", lambda e, ps=ps, o=o, sz=sz, xc=xc, n=n: e.scalar_tensor_tensor(
                    out=xc.ap[:, o:o + sz], in0=ps[:, 0:sz], scalar=gate(n), in1=xc.ap[:, o:o + sz], op0=ALU.mult, op1=ALU.add),
                    [("ps", b), xc.all()], [xc.all()])
            self.dma("sp", self.xT_d[n, :, t0:t0 + T], xc.ap[:, 0:T], [xc.all()], keys)
        return evac

    def load_xtile(self, xin, t0, T, col0=0):
        src = self.xT_d[:, :, t0:t0 + T].rearrange("k p t -> p k t")
        self.dma("sp", xin.ap[:, :, col0:col0 + T], src, self.xkeys(t0, T), [xin.all()])

    def mlp_phase(self, i, tiles):
        A = self.A
        mark = A.top
        TM = max(T for (_, T, _) in tiles)
        aT = A.alloc([64, TM], BF16)
        hT = A.alloc([KC, TM], BF16)
        rstd = A.alloc([TM], F32)
        scr = A.alloc([2 * TM], F32)
        sqb = [A.view(scr.off + j * TM * 2, [TM], BF16) for j in range(2)]
        xcb = [A.view(scr.off + j * TM * 4, [TM], F32) for j in range(2)]
        rb = [A.view(scr.off + j * 2048, [512], F32) for j in range(2)]
        xin = A.view(aT.off, [KC, TM], F32)
        cnt = [0]
        rc = [0]
        modv = self.modv[i]
        for (t0, T, r) in tiles:
            tgs = tgs_of(T)
            self.load_xtile(xin, t0, T)
            self.norm_tile(xin, hT, t0, T, self.g1[i][1], lambda k, r=r: modv.ap[:, 48 + k, r:r + 1], r, rstd, sqb)

            def up_evac(m, parts):
                for (b, o, sz) in parts:
                    ps = self.ps[b]
                    rr = rb[rc[0] % 2]
                    rc[0] += 1
                    self.op("act", lambda e, ps=ps, rr=rr, sz=sz: e.activation(out=rr.ap[:, 0:sz], in_=ps[:, 0:sz], func=AF.Relu),
                            [("ps", b)], [rr.all()])
                    self.op("dve", lambda e, rr=rr, o=o, sz=sz, m=m: e.tensor_tensor(out=aT.ap[:, m, o:o + sz], in0=rr.ap[:, 0:sz],
                                                                                   in1=rr.ap[:, 0:sz], op=ALU.mult),
                            [rr.all()], [aT.sub(m)])
            self.gemm_fm(self.w_up[i], D, 0, DFF, lambda k, o, sz: hT.ap[:, k, o:o + sz], lambda kq: [hT.all()], tgs, up_evac, bg=True)
            self.gemm_fm(self.w_down[i], DFF, 0, D, lambda k, o, sz: aT.ap[:, k, o:o + sz],
                         lambda kq: [aT.sub(kq * KC, KC)], tgs,
                         self.residual_evac(t0, T, lambda n, r=r: modv.ap[:, 80 + n, r:r + 1], xcb, cnt), bg=True)
        self.bg_flush()
        A.top = mark

    def qkv_phase(self, i):
        A = self.A
        j = i // 3
        last = i == 3
        mark = A.top
        TM = 1024
        xin = A.alloc([KC, TM], F32)
        hT = A.alloc([KC, TM], BF16)
        rstd = A.alloc([TM], F32)
        sqb = [A.alloc([TM], BF16) for _ in range(2)]
        tab = A.alloc([4, TM], F32)
        qs = [A.alloc([512], BF16) for _ in range(2)]
        t1 = [A.alloc([512], F32) for _ in range(2)]
        t2 = [A.alloc([512], F32) for _ in range(2)]
        qo = [A.alloc([TM], BF16) for _ in range(2)]
        vsb = [A.alloc([512], BF16) for _ in range(2)]
        modv = self.modv[i]
        rot = self.rot_b
        cn = [0, 0, 0]
        import os
        ntl = int(os.environ.get("QKV_TILES", "99"))
        parts_ = os.environ.get("QKV_PARTS", "qkv")
        rmode = int(os.environ.get("ROPE_MODE", "2"))
        tn = 0
        for b in range(2):
            for (t0, T, s0, r) in ((b * S, 1024, 0, b), (b * S + 1024, 1024, 1024, b), (NLAT + b * L, L, S, 2)):
                tn += 1
                if tn > ntl:
                    continue
                tgs = tgs_of(T)
                self.load_xtile(xin, t0, T)
                self.dma("sp", tab.ap[:, :, 0:T], self.k_rope[:, :, s0:s0 + T].rearrange("f p t -> p f t"), [], [tab.all()])
                self.norm_tile(xin, hT, t0, T, self.g1[i][0], lambda k, r=r: modv.ap[:, k, r:r + 1], r, rstd, sqb)

                def mk_evac(dst, tsel):
                    def evac(m, parts):
                        q_ = qo[cn[0] % 2]
                        cn[0] += 1
                        for (bk, o, sz) in parts:
                            ps = self.ps[bk]
                            s_ = qs[cn[1] % 2]
                            a_ = t1[cn[1] % 2]
                            b_ = t2[cn[1] % 2]
                            cn[1] += 1
                            self.op("act", lambda e, ps=ps, s_=s_, sz=sz: e.activation(out=s_.ap[:, 0:sz], in_=ps[:, 0:sz], func=AF.Copy),
                                    [("ps", bk)], [s_.all()])
                            if rmode == 0:
                                self.op("dve", lambda e, s_=s_, q_=q_, o=o, sz=sz: e.tensor_copy(out=q_.ap[:, o:o + sz], in_=s_.ap[:, 0:sz]),
                                        [s_.all()], [q_.all()])
                                continue
                            b2 = self.bank()
                            ps2 = self.ps[b2]
                            self.op("pe", lambda e, ps2=ps2, s_=s_, sz=sz: e.matmul(ps2[:, 0:sz], rot.ap, s_.ap[:, 0:sz], start=True, stop=True),
                                    [s_.all(), rot.all()], [("ps", b2)])
                            if rmode == 1:
                                self.op("dve", lambda e, ps2=ps2, q_=q_, o=o, sz=sz: e.tensor_copy(out=q_.ap[:, o:o + sz], in_=ps2[:, 0:sz]),
                                        [("ps", b2)], [q_.all()])
                                continue
                            self.op("dve", lambda e, ps=ps, a_=a_, o=o, sz=sz: e.tensor_tensor(out=a_.ap[:, 0:sz], in0=ps[:, 0:sz],
                                                                                           in1=tab.ap[:, tsel, o:o + sz], op=ALU.mult),
                                    [("ps", bk), tab.all(), s_.all()], [a_.all()])
                            self.op("dve", lambda e, ps2=ps2, b_=b_, o=o, sz=sz: e.tensor_tensor(out=b_.ap[:, 0:sz], in0=ps2[:, 0:sz],
                                                                                             in1=tab.ap[:, tsel + 1, o:o + sz], op=ALU.mult),
                                    [("ps", b2), tab.all()], [b_.all()])
                            self.op(self.add_eng, lambda e, a_=a_, b_=b_, q_=q_, o=o, sz=sz: e.tensor_tensor(out=q_.ap[:, o:o + sz], in0=a_.ap[:, 0:sz],
                                                                                                   in1=b_.ap[:, 0:sz], op=ALU.add),
                                    [a_.all(), b_.all()], [q_.all()])
                        self.dma("sp", dst[b, m, :, s0:s0 + T], q_.ap[:, 0:T], [q_.all()], [("qk", id(dst), b, m)])
                    return evac
                rhs = lambda k, o, sz: hT.ap[:, k, o:o + sz]
                rr = lambda kq: [hT.all()]
                if not (last and r == 2) and "q" in parts_:
                    self.gemm_fm(self.attn_wq[j], D, 0, D, rhs, rr, tgs, mk_evac(self.QT_d, 0))
                if "k" in parts_:
                    self.gemm_fm(self.attn_wk[j], D, 0, 512, rhs, rr, tgs, mk_evac(self.KT_d, 2))
                if "v" not in parts_:
                    continue
                sls = []
                for half in range(2):
                    sl = self.wslots[self.wslot_i % self.nslot_w]
                    self.wslot_i += 1
                    src = self.attn_wv[j][:, half * 256:(half + 1) * 256].rearrange("(k p) m -> p k m", p=128)
                    self.dma("pool", sl.ap, src, [], [sl.all()])
                    sls.append(sl)
                for q in range(T // 128):
                    v_ = vsb[cn[2] % 2]
                    cn[2] += 1
                    for half in range(2):
                        bk = self.bank()
                        ps = self.ps[bk]
                        sl = sls[half]

                        def mm(e, ps=ps, sl=sl, q=q):
                            for k in range(KC):
                                ins = e.matmul(ps[:, 0:256], hT.ap[:, k, q * 128:(q + 1) * 128], sl.ap[:, k, :], start=(k == 0), stop=(k == KC - 1))
                            return ins
                        self.op("pe", mm, [sl.all(), hT.all()], [("ps", bk)])
                        self.op("act", lambda e, ps=ps, v_=v_, half=half: e.activation(out=v_.ap[:, half * 256:(half + 1) * 256], in_=ps[:, 0:256],
                                                                                       func=AF.Copy), [("ps", bk)], [v_.all()])
                    self.dma("sp", self.V_d[b, s0 + q * 128:s0 + (q + 1) * 128, :], v_.ap, [v_.all()], [("v", b)])
        A.top = mark

    def attn_core(self, i):
        A = self.A
        j = i // 3
        last = i == 3
        mark = A.top
        KT = [A.alloc([SL], BF16) for _ in range(2)]
        Vh = [A.alloc([18, 128], BF16) for _ in range(2)]
        QT = [A.alloc([4, SL], BF16) for _ in range(2)]
        OT = [A.alloc([4, SL], BF16) for _ in range(2)]
        pT = [A.alloc([512], BF16) for _ in range(3)]
        rec = [A.alloc([512], F32) for _ in range(2)]
        ones, idb, msk, esk = self.ones_b, self.ident_b, self.mask_b, self.esink
        it = 0
        pc = 0
        rcn = 0
        gcount = 0
        nq = S if last else SL
        for b in range(2):
            for hk in range(4):
                kt, vh, qt, ot = KT[it % 2], Vh[it % 2], QT[it % 2], OT[it % 2]
                it += 1
                self.dma("sp", kt.ap, self.KT_d[b, hk], [("qk", id(self.KT_d), b, hk)], [kt.all()])
                self.dma("sp", vh.ap, self.V_d[b, :, hk * 128:(hk + 1) * 128].rearrange("(n p) d -> p n d", p=128), [("v", b)], [vh.all()])
                self.dma("sp", qt.ap[:, :, 0:nq], self.QT_d[b, hk * 4:(hk + 1) * 4, :, 0:nq].rearrange("g p t -> p g t"),
                         [("qk", id(self.QT_d), b, hk * 4 + g) for g in range(4)], [qt.all()])
                for g in range(4):
                    head = hk * 4 + g
                    groups = [(qg * 512, 512, qg) for qg in range(4)] + ([] if last else [(S, L, -1)])
                    for (q0, n, qg) in groups:
                        bo, bd = (0, 1) if gcount % 2 == 0 else (2, 3)
                        gcount += 1
                        pso, psd = self.ps[bo], self.ps[bd]
                        steps = [(S + c * 128, 0, n, None) for c in range(2)]
                        if qg >= 0:
                            i0 = qg * 4
                            for jb in range(max(i0 - 1, 0), min(i0 + 5, 16)):
                                ia, ib = max(jb - 1, i0), min(jb + 2, i0 + 4)
                                steps.append((jb * 128, (ia - i0) * 128, (ib - i0) * 128, (ia - (jb - 1)) * 128))
                        prev_pv = None
                        for si, (kc0, ca, cb, mo) in enumerate(steps):
                            w = cb - ca
                            bs = 4 + pc % 4
                            pss = self.ps[bs]
                            p_ = pT[pc % 3]
                            pc += 1

                            def sc(e, pss=pss, kc0=kc0, ca=ca, w=w, mo=mo, q0=q0, g=g, kt=kt, qt=qt):
                                ins = e.matmul(pss[:, 0:w], kt.ap[:, kc0:kc0 + 128], qt.ap[:, g, q0 + ca:q0 + ca + w], start=True, stop=(mo is None))
                                if mo is not None:
                                    ins = e.matmul(pss[:, 0:w], idb.ap, msk.ap[:, mo:mo + w], start=False, stop=True)
                                return ins
                            self.op("pe", sc, [kt.all(), qt.sub(g), idb.all(), msk.all()], [("ps", bs)])
                            self.op("act", lambda e, pss=pss, p_=p_, w=w: e.activation(out=p_.ap[:, 0:w], in_=pss[:, 0:w], func=AF.Exp),
                                    [("ps", bs)], [p_.all()])
                            first, lastst = si == 0, si == len(steps) - 1
                            tile = kc0 // 128

                            def pv(e, pso=pso, psd=psd, p_=p_, ca=ca, w=w, tile=tile, first=first, lastst=lastst, vh=vh):
                                e.matmul(pso[:, ca:ca + w], vh.ap[:, tile, :], p_.ap[:, 0:w], start=first, stop=lastst, skip_group_check=True)
                                return e.matmul(psd[:, ca:ca + w], ones.ap, p_.ap[:, 0:w], start=first, stop=lastst, skip_group_check=True)
                            if prev_pv is not None:
                                self.op("pe", prev_pv[0], prev_pv[1], [("ps", bo), ("ps", bd)])
                            prev_pv = (pv, [p_.all(), vh.all(), ones.all()])
                        self.op("pe", prev_pv[0], prev_pv[1], [("ps", bo), ("ps", bd)])
                        r_ = rec[rcn % 2]
                        rcn += 1
                        self.op("dve", lambda e, psd=psd, r_=r_, n=n, head=head: e.tensor_scalar(
                            out=r_.ap[:, 0:n], in0=psd[:, 0:n], scalar1=esk.ap[:, j, head:head + 1], scalar2=None, op0=ALU.add),
                            [("ps", bd), esk.all()], [r_.all()])
                        self.op("dve", lambda e, r_=r_, n=n: e.reciprocal(out=r_.ap[:, 0:n], in_=r_.ap[:, 0:n]), [r_.all()], [r_.all()])
                        self.op("dve", lambda e, pso=pso, r_=r_, n=n, q0=q0, g=g, ot=ot: e.tensor_tensor(
                            out=ot.ap[:, g, q0:q0 + n], in0=pso[:, 0:n], in1=r_.ap[:, 0:n], op=ALU.mult),
                            [("ps", bo), r_.all()], [ot.sub(g)])
                self.dma("sp", self.OT_d[b, hk * 4:(hk + 1) * 4, :, 0:nq].rearrange("g p t -> p g t"), ot.ap[:, :, 0:nq], [ot.all()],
                         [("ot", b, hk)])
        A.top = mark

    def wo_phase(self, i):
        A = self.A
        j = i // 3
        last = i == 3
        mark = A.top
        TM = 1024
        ott = [A.alloc([KC, TM], BF16) for _ in range(2)]
        xcb = [A.alloc([TM], F32) for _ in range(2)]
        cnt = [0]
        modv = self.modv[i]
        tiles = [(b * S + h * 1024, 1024, b, [(b, h * 1024, 1024, 0)]) for b in range(2) for h in range(2)]
        if not last:
            tiles.append((NLAT, 2 * L, 2, [(0, S, L, 0), (1, S, L, L)]))
        for ti, (t0, T, r, pieces) in enumerate(tiles):
            o_ = ott[ti % 2]
            for (b, s0, n, c0) in pieces:
                self.dma("sp", o_.ap[:, :, c0:c0 + n], self.OT_d[b, :, :, s0:s0 + n].rearrange("h p t -> p h t"),
                         [("ot", b, hk) for hk in range(4)], [o_.all()])
            self.gemm_fm(self.attn_wo[j], D, 0, D, lambda k, o, sz, o_=o_: o_.ap[:, k, o:o + sz], lambda kq, o_=o_: [o_.all()], tgs_of(T),
                         self.residual_evac(t0, T, lambda n, r=r: modv.ap[:, 32 + n, r:r + 1], xcb, cnt))
        A.top = mark

    def gmlp_phase(self, i, tiles):
        A = self.A
        mark = A.top
        T = 512
        uT = A.alloc([32, T], BF16)
        xin = A.view(uT.off, [KC, T], F32)
        hT = A.alloc([KC, T], BF16)
        vsb = A.alloc([4, GW], BF16)
        rstd = A.alloc([T], F32)
        sqb = [A.alloc([T], BF16) for _ in range(2)]
        xcb = [A.alloc([T], F32) for _ in range(2)]
        wsn = A.alloc([16, 128], F32)
        wsT = A.alloc([16, 128], BF16)
        bsB = A.alloc([16, 128], F32)
        tmp = [A.alloc([4, 128], F32) for _ in range(2)]
        st6 = A.alloc([8, 6], F32)
        mv = A.alloc([2], F32)
        rs = A.alloc([2], F32)
        epsv = self.epsv
        modv = self.modv[i]
        gv, idf = self.gv, self.ident_f
        cnt = [0]
        self.dma("sp", wsn.ap, self.gmlp_w_s[0].rearrange("g p q -> p g q"), [], [wsn.all()])
        self.dma("sp", bsB.ap, self.gmlp_b_s[0:1].broadcast_to([128, 16, 128]), [], [bsB.all()], slow=True)
        for g4 in range(4):
            bk = self.bank()
            ps = self.ps[bk]

            def tp(e, ps=ps, g4=g4):
                for jj in range(4):
                    ins = e.transpose(ps[:, jj * 128:(jj + 1) * 128], wsn.ap[:, g4 * 4 + jj, :], idf.ap)
                return ins
            self.op("pe", tp, [wsn.all(), idf.all()], [("ps", bk)])
            self.op("act", lambda e, ps=ps, g4=g4: e.activation(out=wsT.ap[:, g4 * 4:(g4 + 1) * 4, :], in_=ps[:, :].rearrange("p (a b) -> p a b", a=4),
                                                                func=AF.Copy), [("ps", bk)], [wsT.all()])
        tc = 0
        for (t0, T_, r) in tiles:
            assert T_ == T
            tgs = tgs_of(T)
            self.load_xtile(xin, t0, T)
            self.norm_tile(xin, hT, t0, T, self.g1[i][0], lambda k, r=r: modv.ap[:, k, r:r + 1], r, rstd, sqb)

            def u_evac(m, parts):
                for (bk, o, sz) in parts:
                    ps = self.ps[bk]
                    self.op("act", lambda e, ps=ps, m=m, o=o, sz=sz: e.activation(out=uT.ap[:, m, o:o + sz], in_=ps[:, 0:sz], func=AF.Gelu_apprx_tanh),
                            [("ps", bk)], [uT.sub(m)])
            self.gemm_fm(self.gmlp_w_in[0], D, 0, GW, lambda k, o, sz: hT.ap[:, k, o:o + sz], lambda kq: [hT.all()], tgs, u_evac)
            for cg in range(16):
                sl = self.wslots[self.wslot_i % self.nslot_w]
                self.wslot_i += 1
                src = self.gmlp_w_in[0][:, GW + cg * 256:GW + (cg + 1) * 256].rearrange("(k p) m -> p k m", p=128)
                self.dma("pool", sl.ap, src, [], [sl.all()])
                for sub in range(4):
                    bk = self.bank()
                    ps = self.ps[bk]

                    def mm(e, ps=ps, sl=sl, sub=sub):
                        for k in range(KC):
                            ins = e.matmul(ps[:, 0:256], hT.ap[:, k, sub * 128:(sub + 1) * 128], sl.ap[:, k, :], start=(k == 0), stop=(k == KC - 1))
                        return ins
                    self.op("pe", mm, [sl.all(), hT.all()], [("ps", bk)])
                    self.op("act", lambda e, ps=ps, sub=sub, cg=cg: e.activation(out=vsb.ap[:, sub, cg * 256:(cg + 1) * 256], in_=ps[:, 0:256],
                                                                                 func=AF.Gelu_apprx_tanh), [("ps", bk)], [vsb.sub(sub)])
            for sub in range(4):
                def bst(e, sub=sub):
                    for a8 in range(8):
                        ins = e.bn_stats(out=st6.ap[:, a8, :], in_=vsb.ap[:, sub, a8 * 512:(a8 + 1) * 512])
                    return ins
                self.op("dve", bst, [vsb.sub(sub)], [st6.all()])
                self.op("dve", lambda e: e.bn_aggr(out=mv.ap, in_=st6.ap.rearrange("p a b -> p (a b)")), [st6.all()], [mv.all()])
                self.op("act", lambda e: e.activation(out=rs.ap[:, 0:1], in_=mv.ap[:, 1:2], func=AF.Sqrt, bias=epsv.ap[:, 0:1]),
                        [mv.all(), epsv.all()], [rs.all()])
                self.op("dve", lambda e: e.reciprocal(out=rs.ap[:, 0:1], in_=rs.ap[:, 0:1]), [rs.all()], [rs.all()])
                self.op("dve", lambda e: e.scalar_tensor_tensor(out=rs.ap[:, 1:2], in0=mv.ap[:, 0:1], scalar=-1.0, in1=rs.ap[:, 0:1],
                                                                op0=ALU.mult, op1=ALU.mult), [mv.all(), rs.all()], [rs.all()])
                self.op("dve", lambda e, sub=sub: e.tensor_scalar(out=vsb.ap[:, sub, :], in0=vsb.ap[:, sub, :], scalar1=rs.ap[:, 0:1],
                                                                  scalar2=rs.ap[:, 1:2], op0=ALU.mult, op1=ALU.add),
                        [vsb.sub(sub), rs.all()], [vsb.sub(sub)])
            for sub in range(4):
                for c4 in range(8):
                    bk = self.bank()
                    ps = self.ps[bk]
                    t_ = tmp[tc % 2]
                    tc += 1

                    def sp(e, ps=ps, sub=sub, c4=c4):
                        for jj in range(4):
                            cc = c4 * 4 + jj
                            ins = e.matmul(ps[:, jj * 128:(jj + 1) * 128], vsb.ap[:, sub, cc * 128:(cc + 1) * 128], wsT.ap[:, cc // 2, :],
                                           start=True, stop=True)
                        return ins
                    self.op("pe", sp, [vsb.sub(sub), wsT.all()], [("ps", bk)])
                    for jj in range(4):
                        cc = c4 * 4 + jj
                        self.op("dve", lambda e, ps=ps, t_=t_, jj=jj, cc=cc: e.scalar_tensor_tensor(
                            out=t_.ap[:, jj, :], in0=ps[:, jj * 128:(jj + 1) * 128], scalar=gv.ap[:, cc:cc + 1], in1=bsB.ap[:, cc // 2, :],
                            op0=ALU.mult, op1=ALU.add), [("ps", bk), gv.all(), bsB.all()], [t_.all()])
                    us = uT.ap[:, c4 * 4:(c4 + 1) * 4, sub * 128:(sub + 1) * 128]
                    self.op("pool", lambda e, us=us, t_=t_: e.tensor_tensor(out=us, in0=us, in1=t_.ap, op=ALU.mult),
                            [t_.all(), uT.sub(c4 * 4, 4)], [uT.sub(c4 * 4, 4)])
            self.gemm_fm(self.gmlp_w_out[0], GW, 0, D, lambda k, o, sz: uT.ap[:, k, o:o + sz], lambda kq: [uT.sub(kq * KC, KC)], tgs,
                         self.residual_evac(t0, T, lambda n, r=r: modv.ap[:, 32 + n, r:r + 1], xcb, cnt))
        A.top = mark

    def pool_phase(self, i, tiles):
        A = self.A
        mark = A.top
        TM = 1024
        TT = TM + 16
        xin = A.alloc([KC, TT], F32)
        dT = A.alloc([KC, TM], BF16)
        rstd = A.alloc([TT], F32)
        sqb = [A.alloc([TT], BF16) for _ in range(2)]
        xcb = [A.alloc([TM], F32) for _ in range(2)]
        ta = [A.alloc([TT], F32) for _ in range(2)]
        tb = [A.alloc([TT], F32) for _ in range(2)]
        rcB = A.alloc([4, TM], F32)
        pw = A.alloc([16, 512], BF16)
        modv, gts = self.modv[i], self.gts
        cnt = [0]
        for half in range(2):
            self.dma("pool", pw.ap[:, half * 8:(half + 1) * 8, :], self.pool_w[0, half * 2:(half + 1) * 2].rearrange("g (k p) m -> p (g k) m", p=128),
                     [], [pw.all()])
        for ti, (t0, T, s0, Sq, r) in enumerate(tiles):
            lo = 8 if s0 > 0 else 0
            hi = 8 if s0 + T < Sq else 0
            tgs = tgs_of(T)
            rco = (0 if Sq == S else S) + s0
            self.dma("sp", rcB.ap[:, :, 0:T], self.k_rc[:, rco:rco + T].unsqueeze(0).broadcast_to([128, 4, T]), [], [rcB.all()], slow=True)
            self.load_xtile(xin, t0 - lo, T + lo + hi, col0=8 - lo)
            self.norm_tile(xin, xin, t0, T, self.g1[i][0], lambda k, r=r: modv.ap[:, k, r:r + 1], r, rstd, sqb, ncols=T + lo + hi, col0=8 - lo)
            if lo == 0:
                self.op("dve", lambda e: e.memset(xin.ap[:, :, 0:8], 0.0), [xin.all()], [xin.all()])
            if hi == 0:
                self.op("dve", lambda e, T=T: e.memset(xin.ap[:, :, 8 + T:16 + T], 0.0), [xin.all()], [xin.all()])
            for k in range(KC):
                gi = k // 4
                a_, b_ = (ta[0], tb[0]) if k % 3 != 2 else (ta[1], tb[1])
                h = xin.ap[:, k, :]
                shifts = [1, 2, 4][:gi]
                rl, rh = 8, 8 + T
                rngs = [(rl, rh)]
                for sft in reversed(shifts):
                    rl, rh = rl - sft, rh + sft
                    rngs.append((rl, rh))
                rngs.reverse()
                l0, h0 = rngs[0]
                cur = a_
                eng = "pool" if k % 3 == 2 else "dve"
                self.op(eng, lambda e, cur=cur, h=h, l0=l0, h0=h0: e.tensor_tensor(out=cur.ap[:, l0:h0], in0=h[:, l0 - 1:h0 - 1], in1=h[:, l0:h0], op=ALU.add),
                        [xin.sub(k)], [cur.all()])
                for li, sft in enumerate(shifts):
                    l1, h1 = rngs[li + 1]
                    nxt = b_ if cur is a_ else a_
                    self.op(eng, lambda e, cur=cur, nxt=nxt, l1=l1, h1=h1, sft=sft: e.tensor_tensor(
                        out=nxt.ap[:, l1:h1], in0=cur.ap[:, l1 - sft:h1 - sft], in1=cur.ap[:, l1 + sft:h1 + sft], op=ALU.add),
                        [cur.all()], [nxt.all()])
                    cur = nxt
                self.op(eng, lambda e, cur=cur, gi=gi, T=T: e.tensor_tensor(out=cur.ap[:, 8:8 + T], in0=cur.ap[:, 8:8 + T], in1=rcB.ap[:, gi, 0:T], op=ALU.mult),
                        [cur.all(), rcB.all()], [cur.all()])
                self.op(eng, lambda e, cur=cur, k=k, h=h, T=T: e.tensor_tensor(out=dT.ap[:, k, 0:T], in0=cur.ap[:, 8:8 + T], in1=h[:, 8:8 + T], op=ALU.subtract),
                        [cur.all(), xin.sub(k)], [dT.sub(k)])
            evac = self.residual_evac(t0, T, lambda n, r=r: gts.ap[:, n, r:r + 1], xcb, cnt)
            for n in range(KC):
                gi = n // 4
                banks = [self.bank() for _ in tgs]

                def mm(e, n=n, gi=gi, banks=banks, tgs=tgs):
                    for kl in range(4):
                        for (o, sz), bk in zip(tgs, banks):
                            ins = e.matmul(self.ps[bk][:, 0:sz], pw.ap[:, gi * 4 + kl, (n % 4) * 128:(n % 4 + 1) * 128], dT.ap[:, gi * 4 + kl, o:o + sz],
                                           start=(kl == 0), stop=(kl == 3))
                    return ins
                self.op("pe", mm, [pw.all(), dT.sub(gi * 4, 4)], [("ps", bk) for bk in banks])
                evac(n, [(bk, o, sz) for (o, sz), bk in zip(tgs, banks)])
        A.top = mark

    def mixer_phase(self, i):
        kind = i % 3
        last = i == 3
        if kind == 0:
            sub = getattr(self, "attn_sub", ("qkv", "core", "wo"))
            if "qkv" in sub:
                self.qkv_phase(i)
            if "core" in sub:
                self.attn_core(i)
            if "wo" in sub:
                self.wo_phase(i)
        elif kind == 1:
            tiles = [(t * 512, 512, t // 4) for t in range(8)] + ([] if last else [(NLAT, 512, 2)])
            self.gmlp_phase(i, tiles)
        else:
            tiles = [(b * S + h * 1024, 1024, h * 1024, S, b) for b in range(2) for h in range(2)]
            if not last:
                tiles += [(NLAT + b * L, L, 0, L, 2) for b in range(2)]
            self.pool_phase(i, tiles)

    def final_phase(self):
        A = self.A
        mark = A.top
        T = 512
        xin = [A.alloc([KC, T], F32) for _ in range(2)]
        rstd = A.alloc([T], F32)
        sqb = [A.alloc([T], BF16) for _ in range(2)]
        ost = [A.alloc([D], F32) for _ in range(2)]
        ones, idf, gf = self.ones_b, self.ident_f, self.gfin
        oc = 0
        for ti in range(NLAT // T):
            t0 = ti * T
            xi = xin[ti % 2]
            self.load_xtile(xi, t0, T)
            b0 = self.bank()
            ps0 = self.ps[b0]
            for k in range(KC):
                sq = sqb[k % 2]
                self.op("act", lambda e, sq=sq, k=k, xi=xi: e.activation(out=sq.ap, in_=xi.ap[:, k, :], func=AF.Square), [xi.sub(k)], [sq.all()])
                self.op("pe", lambda e, sq=sq, k=k, ps0=ps0: e.matmul(ps0[:, :], ones.ap, sq.ap, start=(k == 0), stop=(k == KC - 1)),
                        [sq.all(), ones.all()], [("ps", b0)])
            self.op("act", lambda e, ps0=ps0: e.activation(out=rstd.ap, in_=ps0[:, :], func=AF.Sqrt, bias=self.epsD.ap[:, 0:1]),
                    [("ps", b0), self.epsD.all()], [rstd.all()])
            self.op("dve", lambda e: e.reciprocal(out=rstd.ap, in_=rstd.ap), [rstd.all()], [rstd.all()])
            for k in range(KC):
                xs = xi.ap[:, k, :]
                self.op("dve", lambda e, xs=xs, k=k: e.scalar_tensor_tensor(out=xs, in0=xs, scalar=gf.ap[:, k:k + 1], in1=rstd.ap,
                                                                             op0=ALU.mult, op1=ALU.mult),
                        [xi.sub(k), rstd.all(), gf.all()], [xi.sub(k)])
            for q in range(T // 128):
                o_ = ost[oc % 2]
                oc += 1
                for kg in range(4):
                    b = self.bank()
                    ps = self.ps[b]

                    def tp(e, ps=ps, xi=xi, kg=kg, q=q):
                        for j in range(4):
                            ins = e.transpose(ps[:, j * 128:(j + 1) * 128], xi.ap[:, kg * 4 + j, q * 128:(q + 1) * 128], idf.ap)
                        return ins
                    self.op("pe", tp, [xi.all(), idf.all()], [("ps", b)])
                    oo = o_.ap[:, kg * 512:(kg + 1) * 512]
                    if kg % 2 == 0:
                        self.op("act", lambda e, oo=oo, ps=ps: e.activation(out=oo, in_=ps[:, :], func=AF.Copy), [("ps", b)], [o_.all()])
                    else:
                        self.op("dve", lambda e, oo=oo, ps=ps: e.tensor_copy(out=oo, in_=ps[:, :]), [("ps", b)], [o_.all()])
                self.dma("sp", self.out[t0 + q * 128:t0 + (q + 1) * 128, :], o_.ap, [o_.all()], [("out", t0 // 128 + q)])
        A.top = mark

    def finish(self):
        outk = [("out", j) for j in range(NLAT // 128)] if self.do_final else []
        keys = outk + ([("xT", j) for j in range(NTOK // 128)] if self.dbg else [])
        if self.dbg:
            keys += [k for k in self.sc.lw if isinstance(k, tuple) and k[0] in ("qk", "v", "ot")]
        if self.dbg:
            dm = self.nc.dram_tensor("dbg_mod", [4, 128, 288], F32, kind="ExternalOutput").ap()
            for i in range(4):
                self.dma("sp", dm[i], self.modv[i].ap.rearrange("p a b -> p (a b)"), [self.modv[i].all()], [("dbgmod", i)])
                keys.append(("dbgmod", i))
        self.op("sp", lambda e: None, keys, [])
        self.sc.emit(self.nc, self.es)
        self.es.close()


def host_consts():
    ident = np.eye(128, dtype=np.float32)
    rot = np.zeros((128, 128), np.float32)
    for m in range(128):
        if m % 64 < 32:
            rot[m + 32, m] = -1.0
        else:
            rot[m - 32, m] = 1.0
    kk = np.arange(128)[:, None]
    qq = np.arange(128)[None, :]
    mask = np.zeros((128, 384), np.float32)
    mask[:, 0:128] = np.where(kk <= qq, 0.0, NEG)
    mask[:, 256:384] = np.where(qq <= kk, 0.0, NEG)
    t = np.arange(S)
    row = (t // 64).astype(np.float32)
    col = (t % 64).astype(np.float32)
    inv = (np.float32(10000.0) ** (-np.arange(0, 64, 2, dtype=np.float32) / np.float32(64))).astype(np.float32)
    ang = np.stack([row[:, None] * inv, col[:, None] * inv], 1).astype(np.float32)
    cos = np.cos(ang).astype(np.float32)
    sin = np.sin(ang).astype(np.float32)
    cT = np.ones((128, SL), np.float32)
    sT = np.zeros((128, SL), np.float32)
    for d in range(128):
        cT[d, :S] = cos[:, d // 64, d % 32]
        sT[d, :S] = sin[:, d // 64, d % 32]
    sc = np.float32(128 ** -0.5)
    rope = np.stack([cT * sc, sT * sc, cT, sT]).astype(np.float32)
    rc = np.zeros((4, S + L), np.float32)
    for gi, w in enumerate((2, 4, 8, 16)):
        for (o, n) in ((0, S), (S, L)):
            tt = np.arange(n)
            lo = np.clip(tt - w // 2, 0, n)
            hi = np.clip(tt + w // 2, 0, n)
            rc[gi, o:o + n] = 1.0 / (hi - lo).astype(np.float32)
    return dict(k_ident=ident, k_rot=rot, k_mask=mask, k_rope=rope, k_rc=rc)


def make_in_maps(inputs, cores):
    kc = host_consts()
    maps = []
    for c in cores:
        b0 = 2 * c
        cc = np.stack([inputs["c"][b0], inputs["c"][b0 + 1], inputs["c_ctx"]], 0)
        cc = np.ascontiguousarray(cc.T.reshape(KC, 128, 3).transpose(1, 0, 2))
        m = dict(kc)
        m["x"] = np.ascontiguousarray(inputs["x"][b0:b0 + 2].reshape(NLAT, D))
        m["ctx"] = np.ascontiguousarray(inputs["ctx"][b0:b0 + 2].reshape(2 * L, D))
        m["cc"] = cc
        for k in ("w_mod", "b_mod", "g_norm", "w_up", "w_down", "attn_wq", "attn_wk", "attn_wv", "attn_wo", "attn_sink",
                  "gmlp_w_in", "gmlp_g_v", "gmlp_w_s", "gmlp_b_s", "gmlp_w_out", "pool_w", "pool_scale", "g_final"):
            m[k] = np.ascontiguousarray(inputs[k], dtype=np.float32)
        maps.append(m)
    return maps


def build(nlayers=4, dbg=False, do_final=True, stages=("mix", "mlp"), attn_sub=("qkv", "core", "wo")):
    B = Builder(nlayers, dbg, do_final)
    B.attn_sub = attn_sub
    B.setup_consts()
    B.modulation([0])
    B.load_inputs()
    lat = [(0, 1024, 0), (1024, 1024, 0), (2048, 1024, 1), (3072, 1024, 1)]
    ctx = [(NLAT, 512, 2)]
    for i in range(nlayers):
        last = i == 3
        if "mix" in stages:
            B.mixer_phase(i)
        if i + 1 < 4 and (i + 1 < nlayers or dbg):
            B.bg = B.mod_steps(i + 1)
        if "mlp" in stages:
            B.mlp_phase(i, lat + ([] if last else ctx))
        B.bg_flush()
    if dbg:
        for i in range(max(nlayers, 1), 4):
            B.modulation([i])
    if do_final:
        B.final_phase()
    B.finish()
    return B.nc


def kernel(**inputs):
    nc = build()
    maps = make_in_maps(inputs, range(8))
    res = run_bass_kernel_spmd(nc, maps, core_ids=list(range(8)))
    out = np.stack([r["out"].reshape(2, S, D) for r in res.results], 0).reshape(16, S, D)
    return out.astype(np.float32)
```
